# Optimizing a Trainium2 kernel written in Bass

```python
import jax, jax.numpy as jnp
from jax import lax
import numpy as np

D_MODEL = 2048
BATCH = 1
SEQ = 8192
DEPTH = 1

CTX_LEN = 256
GRID_W = 64
HEAD_DIM = 128
N_Q_HEADS = 16
N_KV_HEADS = 4
Q_PER_KV = N_Q_HEADS // N_KV_HEADS
Q_BLOCK = 128
AXIS_DIM = HEAD_DIM // 2
ROPE_THETA = 10000.0
ATTN_SCALE = HEAD_DIM ** -0.5
GMLP_GROUPS = 16
GMLP_WIDTH = 2048
GMLP_GROUP_DIM = GMLP_WIDTH // GMLP_GROUPS
CHUNK = 128
D_FF = 5632
MACARON_WEIGHT = 0.5
N_MOD = 9
EPS = 1e-6

Q_W = N_Q_HEADS * HEAD_DIM
KV_W = N_KV_HEADS * HEAD_DIM
Q_END = Q_W
K_END = Q_END + KV_W
V_END = K_END + KV_W
GV_END = V_END + 2 * GMLP_WIDTH
IN_W = GV_END + 2 * D_MODEL

kernel_name = "hybrid_gqa_gmlp_macaron_dit_layer"


def rmsnorm(x, w):
    xf = x.astype(jnp.float32)
    y = xf * lax.rsqrt(jnp.mean(xf * xf, axis=-1, keepdims=True) + EPS)
    return (y * w.astype(jnp.float32)).astype(x.dtype)


def layernorm(x, w, b):
    xf = x.astype(jnp.float32)
    mu = jnp.mean(xf, axis=-1, keepdims=True)
    xc = xf - mu
    y = xc * lax.rsqrt(jnp.mean(xc * xc, axis=-1, keepdims=True) + EPS)
    return (y * w.astype(jnp.float32) + b.astype(jnp.float32)).astype(x.dtype)


def modulate(h, shift, scale):
    return h * (1 + scale) + shift


def axial_rope(n_rows):
    row = jnp.broadcast_to(jnp.arange(n_rows, dtype=jnp.float32)[:, None], (n_rows, GRID_W)).reshape(-1)
    col = jnp.broadcast_to(jnp.arange(GRID_W, dtype=jnp.float32)[None, :], (n_rows, GRID_W)).reshape(-1)
    inv_freq = ROPE_THETA ** (-jnp.arange(0, AXIS_DIM, 2, dtype=jnp.float32) / AXIS_DIM)
    ang = jnp.concatenate([row[:, None] * inv_freq, col[:, None] * inv_freq], axis=-1)
    return jnp.cos(ang), jnp.sin(ang)


def apply_rope(x, cos, sin):
    B, S, H, Dh = x.shape
    xr = x.astype(jnp.float32).reshape(B, S, H, Dh // 2, 2)
    x1, x2 = xr[..., 0], xr[..., 1]
    cs, sn = cos[None, :, None, :], sin[None, :, None, :]
    out = jnp.stack([x1 * cs - x2 * sn, x1 * sn + x2 * cs], axis=-1)
    return out.reshape(B, S, H, Dh).astype(x.dtype)


def heads_norm(z, n_heads, gain):
    return rmsnorm(z.reshape(z.shape[0], z.shape[1], n_heads, HEAD_DIM), gain)


def gqa_attend(qi, k_all, v_all):
    s = jnp.einsum('bqkgd,bskd->bkgqs', qi, k_all, preferred_element_type=jnp.float32) * ATTN_SCALE
    p = jax.nn.softmax(s, axis=-1).astype(v_all.dtype)
    return jnp.einsum('bkgqs,bskd->bqkgd', p, v_all)


def latent_attention(q, k, v, k_ctx, v_ctx):
    B, S = q.shape[:2]
    k_all = jnp.concatenate([k_ctx, k], axis=1)
    v_all = jnp.concatenate([v_ctx, v], axis=1)
    qb = q.reshape(B, S // Q_BLOCK, Q_BLOCK, N_KV_HEADS, Q_PER_KV, HEAD_DIM).swapaxes(0, 1)
    o = lax.map(lambda qi: gqa_attend(qi, k_all, v_all), qb)
    return o.swapaxes(0, 1).reshape(B, S, Q_W)


def context_attention(q, k, v):
    B, C = q.shape[:2]
    o = gqa_attend(q.reshape(B, C, N_KV_HEADS, Q_PER_KV, HEAD_DIM), k, v)
    return o.reshape(B, C, Q_W)


def gmlp_branch(z_uv, ln_w, ln_b, w_s, b_s):
    B, N, _ = z_uv.shape
    z = jax.nn.gelu(z_uv, approximate=False)
    u, v = z[..., :GMLP_WIDTH], z[..., GMLP_WIDTH:]
    vn = layernorm(v, ln_w, ln_b).reshape(B, N // CHUNK, CHUNK, GMLP_GROUPS, GMLP_GROUP_DIM)
    mixed = jnp.einsum('gpq,bcqgd->bcpgd', w_s, vn) + b_s.T[:, :, None]
    return u * mixed.reshape(B, N, GMLP_WIDTH)


def merge_branches(attn, gm, gate_logits, b_gate_l, w_ba, w_bg, w_o):
    g = jax.nn.sigmoid(gate_logits.reshape(*gate_logits.shape[:-1], 2, D_MODEL) + b_gate_l)
    return (g[..., 0, :] * (attn @ w_ba) + g[..., 1, :] * (gm @ w_bg)) @ w_o


def ffn_sublayer(x, shift, scale, gate, norm_w, w_in, w_out):
    h = modulate(rmsnorm(x, norm_w), shift, scale)
    a, b = jnp.split(h @ w_in, 2, axis=-1)
    return x + MACARON_WEIGHT * gate * ((jax.nn.silu(a) * b) @ w_out)


def _normal(key, shape, scale):
    return jax.random.normal(key, shape, jnp.float32) * scale


def setup_inputs(seed: int = 0) -> dict:
    key = jax.random.key(seed)
    ks = jax.random.split(key, 24)
    L, D, F = DEPTH, D_MODEL, D_FF
    return {
        "x": _normal(ks[0], (BATCH, SEQ, D), 1.0),
        "c": _normal(ks[1], (BATCH, D), 1.0),
        "ctx": _normal(ks[2], (BATCH, CTX_LEN, D), 1.0),
        "c_ctx": _normal(ks[3], (D,), 1.0),
        "w_mod": _normal(ks[4], (L, D, N_MOD * D), 0.5 * D ** -0.5),
        "b_mod": _normal(ks[5], (L, N_MOD * D), 0.02),
        "norm_w": 1.0 + _normal(ks[6], (L, 3, D), 0.05),
        "w_ffn1_in": _normal(ks[7], (L, D, 2 * F), D ** -0.5),
        "w_ffn1_out": _normal(ks[8], (L, F, D), F ** -0.5),
        "w_ffn2_in": _normal(ks[9], (L, D, 2 * F), D ** -0.5),
        "w_ffn2_out": _normal(ks[10], (L, F, D), F ** -0.5),
        "w_in": _normal(ks[11], (L, D, IN_W), D ** -0.5),
        "b_gate": _normal(ks[12], (L, 2, D), 0.1),
        "q_norm_w": 1.0 + _normal(ks[13], (L, HEAD_DIM), 0.05),
        "k_norm_w": 1.0 + _normal(ks[14], (L, HEAD_DIM), 0.05),
        "gmlp_ln_w": 1.0 + _normal(ks[15], (L, GMLP_WIDTH), 0.05),
        "gmlp_ln_b": _normal(ks[16], (L, GMLP_WIDTH), 0.02),
        "w_spatial": _normal(ks[17], (L, GMLP_GROUPS, CHUNK, CHUNK), 0.5 * CHUNK ** -0.5),
        "b_spatial": 1.0 + _normal(ks[18], (L, GMLP_GROUPS, CHUNK), 0.1),
        "w_branch_attn": _normal(ks[19], (L, Q_W, D), Q_W ** -0.5),
        "w_branch_gmlp": _normal(ks[20], (L, GMLP_WIDTH, D), GMLP_WIDTH ** -0.5),
        "w_out": _normal(ks[21], (L, D, D), D ** -0.5),
        "final_norm_w": 1.0 + _normal(ks[22], (D,), 0.05),
    }


def reference(x, c, ctx, c_ctx, w_mod, b_mod, norm_w, w_ffn1_in, w_ffn1_out, w_ffn2_in, w_ffn2_out,
              w_in, b_gate, q_norm_w, k_norm_w, gmlp_ln_w, gmlp_ln_b, w_spatial, b_spatial,
              w_branch_attn, w_branch_gmlp, w_out, final_norm_w):
    B, S, D = x.shape
    rows = S // GRID_W
    cos, sin = axial_rope(rows)
    sc = jax.nn.silu(c)
    scc = jax.nn.silu(c_ctx)
    for l in range(DEPTH):
        mx = (sc @ w_mod[l] + b_mod[l]).reshape(B, N_MOD, 1, D)
        mc = (scc @ w_mod[l] + b_mod[l]).reshape(1, N_MOD, 1, D)

        x = ffn_sublayer(x, mx[:, 0], mx[:, 1], mx[:, 2], norm_w[l, 0], w_ffn1_in[l], w_ffn1_out[l])
        ctx = ffn_sublayer(ctx, mc[:, 0], mc[:, 1], mc[:, 2], norm_w[l, 0], w_ffn1_in[l], w_ffn1_out[l])

        hx = modulate(rmsnorm(x, norm_w[l, 1]), mx[:, 3], mx[:, 4])
        hc = modulate(rmsnorm(ctx, norm_w[l, 1]), mc[:, 3], mc[:, 4])
        zx = hx @ w_in[l]
        qx = apply_rope(heads_norm(zx[..., :Q_END], N_Q_HEADS, q_norm_w[l]), cos, sin)
        kx = apply_rope(heads_norm(zx[..., Q_END:K_END], N_KV_HEADS, k_norm_w[l]), cos, sin)
        vx = zx[..., K_END:V_END].reshape(B, S, N_KV_HEADS, HEAD_DIM)

        zc_kv = hc @ w_in[l][:, Q_END:V_END]
        kc = heads_norm(zc_kv[..., :KV_W], N_KV_HEADS, k_norm_w[l])
        vc = zc_kv[..., KV_W:].reshape(B, hc.shape[1], N_KV_HEADS, HEAD_DIM)

        attn_x = latent_attention(qx, kx, vx, kc, vc)
        gm_x = gmlp_branch(zx[..., V_END:GV_END], gmlp_ln_w[l], gmlp_ln_b[l], w_spatial[l], b_spatial[l])
        y = merge_branches(attn_x, gm_x, zx[..., GV_END:], b_gate[l],
                           w_branch_attn[l], w_branch_gmlp[l], w_out[l])

        if l < DEPTH - 1:
            qc = heads_norm(hc @ w_in[l][:, :Q_END], N_Q_HEADS, q_norm_w[l])
            zc_rest = hc @ w_in[l][:, V_END:]
            attn_c = context_attention(qc, kc, vc)
            gm_c = gmlp_branch(zc_rest[..., :2 * GMLP_WIDTH], gmlp_ln_w[l], gmlp_ln_b[l],
                               w_spatial[l], b_spatial[l])
            yc = merge_branches(attn_c, gm_c, zc_rest[..., 2 * GMLP_WIDTH:], b_gate[l],
                                w_branch_attn[l], w_branch_gmlp[l], w_out[l])
            ctx = ctx + mc[:, 5] * yc
            ctx = ffn_sublayer(ctx, mc[:, 6], mc[:, 7], mc[:, 8], norm_w[l, 2], w_ffn2_in[l], w_ffn2_out[l])

        x = x + mx[:, 5] * y
        x = ffn_sublayer(x, mx[:, 6], mx[:, 7], mx[:, 8], norm_w[l, 2], w_ffn2_in[l], w_ffn2_out[l])
    return rmsnorm(x, final_norm_w)
```

```python
import numpy as np
from contextlib import ExitStack
import concourse.bass as bass
import concourse.mybir as mybir
from concourse.bass_utils import run_bass_kernel_spmd

F32 = mybir.dt.float32
BF16 = mybir.dt.bfloat16
AF = mybir.ActivationFunctionType
ALU = mybir.AluOpType

NCORES = 8
D = 2048
KC = 16
TL = 1024
TCX = 32
T = TL + TCX
FF = 5632
NFC = 44
EPS = 1e-6
HD = 128
ATTN_SCALE = HD ** -0.5
Q_END, K_END, V_END, GV_END = 2048, 2560, 3072, 7168
NVEC = 384
C_BMOD, C_NW, C_FNW, C_BG, C_LNW, C_LNB, C_C, C_CC, C_GQ, C_GQS, C_GK, C_GKS = 0, 144, 192, 208, 240, 256, 272, 288, 304, 305, 306, 307

BLK_ALL = [(0, 512, 0), (512, 512, 0), (1024, 32, 1)]
BLK_LAT = [(0, 512, 0), (512, 512, 0)]


class Trk:
    def __init__(self, nc, es):
        self.nc = nc
        self.es = es
        self.E = {'pe': nc.tensor, 'act': nc.scalar, 'dve': nc.vector, 'pool': nc.gpsimd, 'sp': nc.sync}
        self.esem = {e: es.enter_context(nc.semaphore('s_' + e)) for e in self.E}
        self.ecnt = {e: 0 for e in self.E}
        self.dsem = {}
        self.dcnt = {}
        self.seen = {e: {} for e in self.E}
        self.lastw = {}
        self.readers = {}

    def _sem(self, name):
        return self.esem[name] if name in self.esem else self.dsem[name]

    def _wait(self, e, ev):
        name, val = ev
        if name in self.dcnt:
            val = max(val, self.dcnt[name])
        if self.seen[e].get(name, 0) >= val:
            return
        self.E[e].wait_ge(self._sem(name), val)
        self.seen[e][name] = val

    def _deps(self, e, r, w, skip=None):
        for k in r:
            ev = self.lastw.get(k)
            if ev is not None:
                self._wait(e, ev)
        for k in w:
            ev = self.lastw.get(k)
            if ev is not None and ev[0] != e and ev[0] != skip:
                self._wait(e, ev)
            for nm, val in self.readers.get(k, {}).items():
                if nm != e:
                    self._wait(e, (nm, val))

    def _record(self, ev, r, w):
        for k in r:
            d = self.readers.setdefault(k, {})
            d[ev[0]] = max(d.get(ev[0], 0), ev[1])
        for k in w:
            self.lastw[k] = ev
            self.readers[k] = {}

    def op(self, e, fn, r=(), w=()):
        self._deps(e, r, w)
        ins = fn(self.E[e])
        ins.then_inc(self.esem[e], 1)
        self.ecnt[e] += 1
        self._record((e, self.ecnt[e]), r, w)

    def dma(self, q, out, in_, sem, r=(), w=()):
        self._deps(q, r, w, skip=sem)
        if sem not in self.dsem:
            self.dsem[sem] = self.es.enter_context(self.nc.semaphore('d_' + sem))
            self.dcnt[sem] = 0
        self.E[q].dma_start(out=out, in_=in_).then_inc(self.dsem[sem], 16)
        self.dcnt[sem] += 16
        self._record((sem, self.dcnt[sem]), r, w)

    def coll(self, fn, sem, r=(), w=()):
        q = 'pool'
        self._deps(q, r, w)
        if sem not in self.dsem:
            self.dsem[sem] = self.es.enter_context(self.nc.semaphore('d_' + sem))
            self.dcnt[sem] = 0
        fn(self.E[q]).then_inc(self.dsem[sem])
        self.dcnt[sem] += 1
        self._record((sem, self.dcnt[sem]), r, w)

    def barrier(self):
        evs = [(e, self.ecnt[e]) for e in self.E if self.ecnt[e] > 0]
        evs += [(s, self.dcnt[s]) for s in self.dsem if self.dcnt[s] > 0]
        for e in self.E:
            for ev in evs:
                if ev[0] != e:
                    self._wait(e, ev)
        self.lastw = {}
        self.readers = {}


def build_nc(stage=9, debug=False, mode='F'):
    nc = bass.Bass("TRN2", target_bir_lowering=False)

    def din(name, shape, dt=F32):
        return nc.dram_tensor(name, list(shape), dt, kind="ExternalInput").ap()

    if mode != 'B':
        x_loc = din("x_loc", [TL, D])
        ctx_loc = din("ctx_loc", [TCX, D])
    vecs = din("vecs", [NVEC, 128])
    if mode != 'B':
        w_mod = din("w_mod", [D, 9 * D])
        w1i = din("w_ffn1_in", [D, 2 * FF])
        w1o = din("w_ffn1_out", [FF, D])
    w_in = din("w_in", [D, 11264])
    if mode != 'A':
        w2i = din("w_ffn2_in", [D, 2 * FF])
        w2o = din("w_ffn2_out", [FF, D])
        w_ba = din("w_branch_attn", [D, D])
        w_bg = din("w_branch_gmlp", [D, D])
        w_o = din("w_out", [D, D])
        w_sp = din("w_sp2", [128, 2048])
        b_sp = din("b_sp_bc", [128, 2048])
    cosT = din("cosT", [128, T])
    sinT = din("sinT", [128, T])
    ident_d = din("ident", [128, 128])
    perm_d = din("perm", [128, 128])
    if mode != 'A':
        out_loc = nc.dram_tensor("out_loc", [TL, D], F32, kind="ExternalOutput").ap()
    dbg = {}
    if debug:
        dbg['xT'] = nc.dram_tensor("dbg_xT", [128, KC * T], F32, kind="ExternalOutput").ap()
        dbg['hT'] = nc.dram_tensor("dbg_hT", [128, KC * T], BF16, kind="ExternalOutput").ap()
        dbg['modT'] = nc.dram_tensor("dbg_modT", [128, 288], F32, kind="ExternalOutput").ap()
        dbg['qT'] = nc.dram_tensor("dbg_qT", [128, KC * TL], BF16, kind="ExternalOutput").ap()
        dbg['kT'] = nc.dram_tensor("dbg_kT", [128, 4 * T], BF16, kind="ExternalOutput").ap()
        dbg['pgT'] = nc.dram_tensor("dbg_pgT", [128, KC * TL], BF16, kind="ExternalOutput").ap()
        dbg['attnT'] = nc.dram_tensor("dbg_attnT", [128, KC * TL], BF16, kind="ExternalOutput").ap()

    io_kind = {'F': "Internal", 'A': "ExternalOutput", 'B': "ExternalInput"}[mode]
    x_spill = nc.dram_tensor("x_spill", [128, KC * T], F32, kind=io_kind).ap()
    if mode == 'A':
        modT_o = nc.dram_tensor("modT_o", [128, 288], F32, kind="ExternalOutput").ap()
    if mode == 'B':
        modT_i = nc.dram_tensor("modT_i", [128, 288], F32, kind="ExternalInput").ap()
    q_spill = nc.dram_tensor("q_spill", [128, KC * TL], BF16).ap()
    pg_spill = nc.dram_tensor("pg_spill", [128, KC * TL], BF16).ap()
    if mode != 'B':
        kst = nc.dram_tensor("kst", [512, T], BF16, kind=("ExternalOutput" if mode == 'A' else "Internal")).ap()
        vst = nc.dram_tensor("vst", [T, 512], BF16, kind=("ExternalOutput" if mode == 'A' else "Internal")).ap()
    if mode != 'A':
        kall = nc.dram_tensor("kall", [NCORES * 512, T], BF16, kind=("ExternalInput" if mode == 'B' else "Internal")).ap()
        vall = nc.dram_tensor("vall", [NCORES * T, 512], BF16, kind=("ExternalInput" if mode == 'B' else "Internal")).ap()

    with ExitStack() as es:
        MEMW = 52480
        M = es.enter_context(nc.sbuf_tensor("M", [128, MEMW], F32))
        PS = [es.enter_context(nc.psum_tensor("ps%d" % i, [128, 512], F32)) for i in range(8)]
        tk = Trk(nc, es)

        def carve(off, shape, dt=F32):
            sz = 4 if dt == F32 else 2
            n = int(np.prod(shape[1:])) * sz
            assert off % 4 == 0 and n % 4 == 0, (off, shape)
            assert off + n <= MEMW * 4, ("SBUF overflow", off, shape)
            v = M[0:shape[0], off // 4:(off + n) // 4]
            if dt != F32:
                v = v.bitcast(dt)
            if len(shape) == 3:
                v = v.rearrange("p (a b) -> p a b", a=shape[1])
            elif len(shape) == 4:
                v = v.rearrange("p (a b c) -> p a b c", a=shape[1], b=shape[2])
            return v

        class Map:
            def __init__(self, base):
                self.off = base

            def get(self, shape, dt=F32):
                sz = 4 if dt == F32 else 2
                n = int(np.prod(shape[1:])) * sz
                n = (n + 63) // 64 * 64
                v = carve(self.off, shape, dt)
                self.off += n
                return v

        cm = Map(0)
        ident = cm.get([128, 128])
        perm = cm.get([128, 128])
        ones_f = cm.get([128, 128])
        ones_b = cm.get([128, 128], BF16)
        vecsT = cm.get([128, NVEC])
        modT = cm.get([128, 144, 2])
        PP = cm.get([128, 9, 16])
        scT = cm.get([128, 16, 2], BF16)
        assert cm.off <= 6144
        HT_OFF = 6144
        hT = carve(HT_OFF, [128, KC, T], BF16)
        PB = HT_OFF + KC * T * 2

        def vcol(c):
            return vecsT[:, c:c + 1]

        m0 = Map(PB)
        vstage = m0.get([128, 3, 128])
        wm = [m0.get([128, KC, 512], BF16) for _ in range(2)]

        tk.dma('sp', ident, ident_d[:, :], 'c0', w=['ident'])
        tk.dma('sp', perm, perm_d[:, :], 'c0', w=['perm'])
        tk.dma('sp', vstage, vecs.rearrange("(a p) c -> p a c", p=128), 'c0', w=['vstage'])
        tk.op('dve', lambda e: e.memset(ones_f, 1.0), w=['ones_f'])
        tk.op('dve', lambda e: e.memset(ones_b, 1.0), w=['ones_b'])
        for a in range(3):
            tk.op('pe', lambda e, a=a: e.transpose(out=PS[0][:, a * 128:(a + 1) * 128], in_=vstage[:, a, :], identity=ident),
                  r=['vstage', 'ident'], w=['ps0'])
        tk.op('dve', lambda e: e.tensor_copy(out=vecsT, in_=PS[0][:, 0:NVEC]), r=['ps0'], w=['vecsT'])
        for v, c0 in ((0, C_C), (1, C_CC)):
            tk.op('act', lambda e, v=v, c0=c0: e.activation(out=scT[:, :, v], in_=vecsT[:, c0:c0 + 16], func=AF.Silu),
                  r=['vecsT'], w=['scT'])

        def wload(dst, src, sem, key):
            tk.dma('pool', dst, src, sem, w=[key])

        def colslot_src(wap, c0, n=512):
            return wap[:, c0:c0 + n].rearrange("(k p) c -> p k c", p=128)

        if mode == 'B':
            tk.dma('sp', modT.rearrange("p j v -> p (j v)"), modT_i[:, :], 'c1', w=['modT'])
        else:
            wload(wm[0], colslot_src(w_mod, 0), 'wm0', 'wm0')
        for cb in range(36 if mode != 'B' else 0):
            if cb + 1 < 36:
                s = (cb + 1) % 2
                wload(wm[s], colslot_src(w_mod, (cb + 1) * 512), 'wm%d' % s, 'wm%d' % s)
            s = cb % 2
            for jj in range(4):
                j = cb * 4 + jj

                def mm(e, s=s, jj=jj, j=j):
                    for k in range(KC):
                        ins = e.matmul(PS[1][:, 2 * j:2 * j + 2], lhsT=wm[s][:, k, jj * 128:(jj + 1) * 128],
                                       rhs=scT[:, k, :], start=(k == 0), stop=(k == KC - 1))
                    return ins
                tk.op('pe', mm, r=['wm%d' % s, 'scT'], w=['ps1'])
        psm = PS[1][:, 0:288].rearrange("p (j v) -> p j v", v=2)
        for v in range(2 if mode != 'B' else 0):
            tk.op('dve', lambda e, v=v: e.tensor_tensor(out=modT[:, :, v], in0=psm[:, :, v], in1=vecsT[:, C_BMOD:C_BMOD + 144], op=ALU.add),
                  r=['ps1', 'vecsT'], w=['modT'])
        if mode == 'A':
            tk.dma('sp', modT_o[:, :], modT.rearrange("p j v -> p (j v)"), 'c1', r=['modT'])

        def mk_A(idx, modi, v, nwi):
            tk.op('dve', lambda e: e.scalar_tensor_tensor(out=PP[:, idx, :], in0=modT[:, modi * 16:(modi + 1) * 16, v], scalar=1.0,
                                                          in1=vecsT[:, C_NW + nwi * 16:C_NW + (nwi + 1) * 16], op0=ALU.add, op1=ALU.mult),
                  r=['modT', 'vecsT'], w=['PP'])

        def mk_G(idx, modi, v, coef):
            tk.op('dve', lambda e: e.tensor_scalar(out=PP[:, idx, :], in0=modT[:, modi * 16:(modi + 1) * 16, v], scalar1=coef, scalar2=None,
                                                   op0=ALU.mult), r=['modT'], w=['PP'])
        mk_A(0, 1, 0, 0); mk_A(1, 1, 1, 0); mk_G(2, 2, 0, 0.5); mk_G(3, 2, 1, 0.5)
        mk_A(4, 4, 0, 1); mk_A(5, 4, 1, 1); mk_G(6, 5, 0, 1.0)
        mk_A(7, 7, 0, 2); mk_G(8, 8, 0, 0.5)

        def Pcol(idx, k):
            return PP[:, idx, k:k + 1]

        def Mcol(modi, k, v):
            return modT[:, modi * 16 + k, v:v + 1]

        if debug:
            tk.dma('sp', dbg['modT'], modT.rearrange("p j v -> p (j v)"), 'dbg', r=['modT'])
        tk.barrier()

        mf = Map(PB)
        xT = mf.get([128, KC, T])
        gT = [mf.get([128, 4, T], BF16) for _ in range(2)]
        wi = [mf.get([128, KC, 2, 256], BF16) for _ in range(2)]
        wo = [mf.get([128, 2, 2048], BF16) for _ in range(2)]
        sqb = [mf.get([128, 512], BF16) for _ in range(2)]
        rstd = mf.get([128, T])
        lnv = mf.get([128, T])
        tmpA = [mf.get([128, 512]) for _ in range(2)]
        sT = [mf.get([128, 512]) for _ in range(2)]
        xstage = [mf.get([128, D]) for _ in range(2)]
        xn = carve(PB + KC * T * 4 + 2 * 4 * T * 2, [128, KC, 128])
        ostage = [carve(PB + KC * T * 4 + 2 * 4 * T * 2 + 8192 * (1 + i), [128, D]) for i in range(2)]

        def load_x():
            for i in range(9):
                n = 128 if i < 8 else TCX
                t0 = i * 128
                st = xstage[i % 2]
                src = x_loc[t0:t0 + 128, :] if i < 8 else ctx_loc[:, :]
                tk.dma('sp', st[0:n, :], src, 'xs%d' % (i % 2), w=['xst%d' % (i % 2)])
                for c in range(4):
                    b = c
                    def tr(e, c=c, n=n, st=st, b=b):
                        for j in range(4):
                            kk = 4 * c + j
                            ins = e.transpose(out=PS[b][:, j * 128:j * 128 + n], in_=st[0:n, kk * 128:(kk + 1) * 128], identity=ident[0:n, 0:n])
                        return ins
                    tk.op('pe', tr, r=['xst%d' % (i % 2), 'ident'], w=['ps%d' % b])
                    psv = PS[b][:, :].rearrange("p (a t) -> p a t", a=4)[:, :, 0:n]
                    dst = xT[:, 4 * c:4 * c + 4, t0:t0 + n]
                    if c % 2 == 0:
                        tk.op('act', lambda e, psv=psv, dst=dst: e.activation(out=dst, in_=psv, func=AF.Copy), r=['ps%d' % b], w=[('xT', i)])
                    else:
                        tk.op('dve', lambda e, psv=psv, dst=dst: e.tensor_copy(out=dst, in_=psv), r=['ps%d' % b], w=[('xT', i)])

        def xkeys(t0, n):
            return [('xT', i) for i in range(t0 // 128, (t0 + n + 127) // 128)]

        def rms_rstd(blocks, nchunks=KC):
            for (t0, n, v) in blocks:
                for k in range(KC):
                    sq = sqb[k % 2]
                    tk.op('act', lambda e, k=k, sq=sq: e.activation(out=sq[:, 0:n], in_=xT[:, k, t0:t0 + n], func=AF.Square),
                          r=xkeys(t0, n), w=['sqb%d' % (k % 2)])
                    tk.op('pe', lambda e, k=k, sq=sq: e.matmul(PS[7][:, 0:n], lhsT=ones_b, rhs=sq[:, 0:n], start=(k == 0), stop=(k == KC - 1)),
                          r=['sqb%d' % (k % 2), 'ones_b'], w=['ps7'])
                tk.op('act', lambda e: e.activation(out=lnv[:, t0:t0 + n], in_=PS[7][:, 0:n], func=AF.Ln, bias=EPS, scale=1.0 / D),
                      r=['ps7'], w=[('lnv', t0)])
                tk.op('act', lambda e: e.activation(out=rstd[:, t0:t0 + n], in_=lnv[:, t0:t0 + n], func=AF.Exp, scale=-0.5),
                      r=[('lnv', t0)], w=[('rstd', t0)])

        def norm_mod(blocks, Aidx, Bmod):
            rms_rstd(blocks)
            for (t0, n, v) in blocks:
                for k in range(KC):
                    tmp = tmpA[k % 2]
                    tk.op('dve', lambda e, k=k, tmp=tmp: e.tensor_tensor(out=tmp[:, 0:n], in0=xT[:, k, t0:t0 + n], in1=rstd[:, t0:t0 + n], op=ALU.mult),
                          r=xkeys(t0, n) + [('rstd', t0)], w=['tmpA%d' % (k % 2)])
                    tk.op('act', lambda e, k=k, tmp=tmp: e.activation(out=hT[:, k, t0:t0 + n], in_=tmp[:, 0:n], func=AF.Identity,
                                                                     bias=Mcol(Bmod, k, v), scale=Pcol(Aidx[v], k)),
                          r=['tmpA%d' % (k % 2), 'PP', 'modT'], w=[('hT', t0)])

        def ffn(wI, wO, Gidx, blocks):
            hkeys = [('hT', b[0]) for b in blocks]

            def load_wi(u):
                s = u % 2
                tk.dma('pool', wi[s][:, :, 0, :], colslot_src(wI, 256 * u, 256), 'wi%d' % s, w=['wi%d' % s])
                tk.dma('pool', wi[s][:, :, 1, :], colslot_src(wI, FF + 256 * u, 256), 'wi%d' % s, w=['wi%d' % s])

            def load_wo(u):
                s = u % 2
                tk.dma('pool', wo[s], wO[256 * u:256 * u + 256, :].rearrange("(f p) c -> p f c", p=128), 'wo%d' % s, w=['wo%d' % s])

            def in_proj(u):
                s = u % 2
                g = u // 2
                for fi in range(2):
                    for bi, (t0, n, v) in enumerate(blocks):
                        st = (fi * len(blocks) + bi) % 2
                        ba, bb = 2 * st, 2 * st + 1
                        for ab, bk in ((0, ba), (1, bb)):
                            def mm(e, ab=ab, bk=bk):
                                for k in range(KC):
                                    ins = e.matmul(PS[bk][:, 0:n], lhsT=wi[s][:, k, ab, fi * 128:(fi + 1) * 128], rhs=hT[:, k, t0:t0 + n],
                                                   start=(k == 0), stop=(k == KC - 1))
                                return ins
                            tk.op('pe', mm, r=['wi%d' % s, ('hT', t0)], w=['ps%d' % bk])
                        tk.op('act', lambda e, st=st, ba=ba: e.activation(out=sT[st][:, 0:n], in_=PS[ba][:, 0:n], func=AF.Silu),
                              r=['ps%d' % ba], w=['sT%d' % st])
                        fl = (u % 2) * 2 + fi
                        tk.op('dve', lambda e, st=st, bb=bb, fl=fl: e.tensor_tensor(out=gT[g % 2][:, fl, t0:t0 + n], in0=sT[st][:, 0:n],
                                                                                    in1=PS[bb][:, 0:n], op=ALU.mult),
                              r=['sT%d' % st, 'ps%d' % bb], w=[('gT', g % 2, t0)])

            ycnt = [0]

            def out_proj(g):
                for dch in range(KC):
                    for (t0, n, v) in blocks:
                        bk = 4 + ycnt[0] % 4
                        ycnt[0] += 1

                        def mm(e, bk=bk):
                            for fl in range(4):
                                ins = e.matmul(PS[bk][:, 0:n], lhsT=wo[fl // 2][:, fl % 2, dch * 128:(dch + 1) * 128], rhs=gT[g % 2][:, fl, t0:t0 + n],
                                               start=(fl == 0), stop=(fl == 3))
                            return ins
                        tk.op('pe', mm, r=['wo0', 'wo1', ('gT', g % 2, t0)], w=['ps%d' % bk])
                        tk.op('dve', lambda e, bk=bk: e.scalar_tensor_tensor(out=xT[:, dch, t0:t0 + n], in0=PS[bk][:, 0:n], scalar=Pcol(Gidx[v], dch),
                                                                            in1=xT[:, dch, t0:t0 + n], op0=ALU.mult, op1=ALU.add),
                              r=['ps%d' % bk, 'PP'] + xkeys(t0, n), w=xkeys(t0, n))

            NU = NFC // 2
            load_wi(0); load_wi(1); load_wo(0); load_wo(1)
            in_proj(0); load_wi(2)
            in_proj(1); load_wi(3)
            for g in range(NU // 2):
                if g + 1 < NU // 2:
                    for u in (2 * g + 2, 2 * g + 3):
                        in_proj(u)
                        if u + 2 < NU:
                            load_wi(u + 2)
                out_proj(g)
                if g + 1 < NU // 2:
                    load_wo(2 * g + 2); load_wo(2 * g + 3)

        def final_out():
            rms_rstd(BLK_LAT)
            for i in range(8):
                t0 = i * 128
                for k in range(KC):
                    tk.op('dve', lambda e, k=k: e.scalar_tensor_tensor(out=xn[:, k, :], in0=xT[:, k, t0:t0 + 128], scalar=vcol(C_FNW + k),
                                                                       in1=rstd[:, t0:t0 + 128], op0=ALU.mult, op1=ALU.mult),
                          r=xkeys(t0, 128) + [('rstd', (t0 // 512) * 512), 'vecsT'], w=[('xn', k // 4)])
                for c in range(4):
                    b = c
                    def tr(e, c=c, b=b):
                        for j in range(4):
                            kk = 4 * c + j
                            ins = e.transpose(out=PS[b][:, j * 128:(j + 1) * 128], in_=xn[:, kk, :], identity=ident)
                        return ins
                    tk.op('pe', tr, r=[('xn', c), 'ident'], w=['ps%d' % b])
                    dst = ostage[i % 2][:, c * 512:(c + 1) * 512]
                    if c % 2 == 0:
                        tk.op('act', lambda e, dst=dst, b=b: e.activation(out=dst, in_=PS[b][:, :], func=AF.Copy), r=['ps%d' % b], w=['ost%d' % (i % 2)])
                    else:
                        tk.op('dve', lambda e, dst=dst, b=b: e.tensor_copy(out=dst, in_=PS[b][:, :]), r=['ps%d' % b], w=['ost%d' % (i % 2)])
                tk.dma('sp', out_loc[t0:t0 + 128, :], ostage[i % 2], 'out', r=['ost%d' % (i % 2)])

        if mode == 'B':
            tk.dma('sp', xT.rearrange("p k t -> p (k t)"), x_spill, 'ld', w=[('xT', i) for i in range(9)])
        else:
            load_x()
            norm_mod(BLK_ALL, (0, 1), 0)
            ffn(w1i, w1o, (2, 3), BLK_ALL)
        if debug:
            tk.dma('sp', dbg['xT'], xT.rearrange("p k t -> p (k t)"), 'dbg', r=[('xT', i) for i in range(9)])

        for _once in ((0,) if stage >= 2 else ()):
            norm_mod(BLK_ALL, (4, 5), 3)
            if debug:
                tk.dma('sp', dbg['hT'], hT.rearrange("p k t -> p (k t)"), 'dbg', r=[('hT', b[0]) for b in BLK_ALL])
            if mode != 'B':
                tk.dma('sp', x_spill, xT.rearrange("p k t -> p (k t)"), 'spill', r=[('xT', i) for i in range(9)], w=['x_spill'])
            tk.barrier()

            ma = Map(PB)
            slots = [ma.get([128, KC, 512], BF16) for _ in range(4)]
            cosS = ma.get([128, T]); sinS = ma.get([128, T])
            gCk = ma.get([128, T]); gSk = ma.get([128, T])
            gCq = ma.get([128, TL]); gSq = ma.get([128, TL])
            kTloc = ma.get([128, 4, T], BF16)
            vloc = ma.get([128, 9, 512], BF16)
            qT = ma.get([128, KC, TL], BF16)
            qf = [ma.get([128, 512]) for _ in range(2)]
            sqf = [ma.get([128, 512]) for _ in range(2)]
            lq = [ma.get([128, 512]) for _ in range(2)]
            rq = [ma.get([128, 512]) for _ in range(2)]
            t1 = [ma.get([128, 512]) for _ in range(2)]
            t2 = [ma.get([128, 512]) for _ in range(2)]

            slot_n = [0]

            def next_slot(src):
                s = slot_n[0] % len(slots)
                slot_n[0] += 1
                wload(slots[s], src, 'sl%d' % s, 'sl%d' % s)
                return s

            tk.dma('sp', cosS, cosT[:, :], 'c0', w=['cosS'])
            tk.dma('sp', sinS, sinT[:, :], 'c0', w=['sinS'])
            if mode != 'B':
                sK = next_slot(colslot_src(w_in, Q_END))
                sV = next_slot(colslot_src(w_in, K_END))
            sQ = []
            if mode != 'A':
                sQ = [next_slot(colslot_src(w_in, 0)), next_slot(colslot_src(w_in, 512))]
            for (dst, src, col, nn, key) in ((gCk, cosS, C_GK, T, 'gCk'), (gSk, sinS, C_GKS, T, 'gSk'), (gCq, cosS, C_GQ, TL, 'gCq'), (gSq, sinS, C_GQS, TL, 'gSq')):
                tk.op('dve', lambda e, dst=dst, src=src, col=col, nn=nn: e.tensor_scalar(out=dst[:, 0:nn], in0=src[:, 0:nn], scalar1=vcol(col), scalar2=None, op0=ALU.mult),
                      r=['cosS', 'sinS', 'vecsT'], w=[key])

            qk_n = [0]

            def qk_head(s, hcol, blocks, gC, gS, gkeys, dst_fn, dkey):
                for (t0, n, v) in blocks:
                    i = qk_n[0] % 2
                    qk_n[0] += 1
                    bq, bs, br = (0, 1, 2) if i == 0 else (3, 4, 5)

                    def mm(e):
                        for k in range(KC):
                            ins = e.matmul(PS[bq][:, 0:n], lhsT=slots[s][:, k, hcol * 128:(hcol + 1) * 128], rhs=hT[:, k, t0:t0 + n],
                                           start=(k == 0), stop=(k == KC - 1))
                        return ins
                    tk.op('pe', mm, r=['sl%d' % s, ('hT', t0)], w=['ps%d' % bq])
                    tk.op('act', lambda e: e.activation(out=qf[i][:, 0:n], in_=PS[bq][:, 0:n], func=AF.Copy), r=['ps%d' % bq], w=['qf%d' % i])
                    tk.op('act', lambda e: e.activation(out=sqf[i][:, 0:n], in_=PS[bq][:, 0:n], func=AF.Square), r=['ps%d' % bq], w=['sqf%d' % i])
                    tk.op('pe', lambda e: e.matmul(PS[bs][:, 0:n], lhsT=ones_f, rhs=sqf[i][:, 0:n], start=True, stop=True),
                          r=['sqf%d' % i, 'ones_f'], w=['ps%d' % bs])
                    tk.op('pe', lambda e: e.matmul(PS[br][:, 0:n], lhsT=perm, rhs=qf[i][:, 0:n], start=True, stop=True),
                          r=['qf%d' % i, 'perm'], w=['ps%d' % br])
                    tk.op('act', lambda e: e.activation(out=lq[i][:, 0:n], in_=PS[bs][:, 0:n], func=AF.Ln, bias=EPS, scale=1.0 / HD),
                          r=['ps%d' % bs], w=['lq%d' % i])
                    tk.op('act', lambda e: e.activation(out=rq[i][:, 0:n], in_=lq[i][:, 0:n], func=AF.Exp, scale=-0.5),
                          r=['lq%d' % i], w=['rq%d' % i])
                    tk.op('dve', lambda e: e.tensor_tensor(out=t1[i][:, 0:n], in0=qf[i][:, 0:n], in1=gC[:, t0:t0 + n], op=ALU.mult),
                          r=['qf%d' % i, gkeys[0]], w=['t1%d' % i])
                    tk.op('dve', lambda e: e.tensor_tensor(out=t2[i][:, 0:n], in0=PS[br][:, 0:n], in1=gS[:, t0:t0 + n], op=ALU.mult),
                          r=['ps%d' % br, gkeys[1]], w=['t2%d' % i])
                    tk.op('dve', lambda e: e.tensor_tensor(out=t1[i][:, 0:n], in0=t1[i][:, 0:n], in1=t2[i][:, 0:n], op=ALU.add),
                          r=['t1%d' % i, 't2%d' % i], w=['t1%d' % i])
                    tk.op('dve', lambda e: e.tensor_tensor(out=dst_fn(t0, n), in0=t1[i][:, 0:n], in1=rq[i][:, 0:n], op=ALU.mult),
                          r=['t1%d' % i, 'rq%d' % i], w=[dkey])

            for h in range(4 if mode != 'B' else 0):
                qk_head(sK, h, BLK_ALL, gCk, gSk, ('gCk', 'gSk'), lambda t0, n, h=h: kTloc[:, h, t0:t0 + n], 'kTloc')
            if mode != 'B':
                tk.dma('sp', kst.rearrange("(h d) t -> d h t", d=128), kTloc, 'kv', r=['kTloc'], w=['kst'])
            if debug and mode != 'B':
                tk.dma('sp', dbg['kT'], kTloc.rearrange("p h t -> p (h t)"), 'dbg', r=['kTloc'])
            for i in range(9 if mode != 'B' else 0):
                n = 128 if i < 8 else TCX
                t0 = i * 128
                bk = 6 + i % 2

                def mmv(e, bk=bk, n=n, t0=t0):
                    for k in range(KC):
                        ins = e.matmul(PS[bk][0:n, :], lhsT=hT[:, k, t0:t0 + n], rhs=slots[sV][:, k, :], start=(k == 0), stop=(k == KC - 1))
                    return ins
                tk.op('pe', mmv, r=['sl%d' % sV, ('hT', (t0 // 512) * 512)], w=['ps%d' % bk])
                tk.op('act', lambda e, bk=bk, n=n, i=i: e.activation(out=vloc[0:n, i, :], in_=PS[bk][0:n, :], func=AF.Copy), r=['ps%d' % bk], w=['vloc'])
            if mode != 'B':
                tk.dma('sp', vst[0:TL, :].rearrange("(i p) c -> p i c", p=128), vloc[:, 0:8, :], 'kv', r=['vloc'], w=['vst'])
                tk.dma('sp', vst[TL:T, :], vloc[0:TCX, 8, :], 'kv', r=['vloc'], w=['vst'])
            if mode != 'A':
                sQ.append(next_slot(colslot_src(w_in, 1024)))
                sQ.append(next_slot(colslot_src(w_in, 1536)))
            if mode == 'F':
                rg = [list(range(NCORES))]
                tk.coll(lambda e: e.collective_compute("AllGather", ALU.bypass, replica_groups=rg, ins=[kst[:, :]], outs=[kall[:, :]]),
                        'cck', r=['kst'], w=['kall'])
                tk.coll(lambda e: e.collective_compute("AllGather", ALU.bypass, replica_groups=rg, ins=[vst[:, :]], outs=[vall[:, :]]),
                        'ccv', r=['vst'], w=['vall'])
            for h in range(16 if mode != 'A' else 0):
                qk_head(sQ[h // 4], h % 4, BLK_LAT, gCq, gSq, ('gCq', 'gSq'), lambda t0, n, h=h: qT[:, h, t0:t0 + n], 'qT')
            if mode != 'A':
                tk.dma('sp', q_spill, qT.rearrange("p h t -> p (h t)"), 'spill', r=['qT'], w=['q_spill'])
            if debug and mode != 'A':
                tk.dma('sp', dbg['qT'], qT.rearrange("p h t -> p (h t)"), 'dbg', r=['qT'])
            tk.barrier()
            if mode == 'A':
                break

            if stage < 3.05:
                break
            mb = Map(PB)
            slots = [mb.get([128, KC, 512], BF16) for _ in range(3)]
            slot_n[0] = 0
            uT = mb.get([128, KC, TL], BF16)
            gvb = mb.get([128, 8, D], BF16)
            pg_off = mb.off
            pgT = mb.get([128, KC, TL], BF16)
            wsT_b = mb.get([128, 16, 128], BF16)
            BB = mb.get([128, 16, 128])
            stats = mb.get([128, 8, 4, 6])
            mv = mb.get([128, 8, 2])
            lnr = mb.get([128, 8]); rsd = mb.get([128, 8]); nbb = mb.get([128, 8])
            sg = [mb.get([128, 512]) for _ in range(2)]
            tg = [mb.get([128, 128]) for _ in range(2)]
            mp = Map(pg_off)
            wsp = mp.get([128, 16, 128])
            wsT_f = mp.get([128, 16, 128])
            bs_all = mp.get([128, 2048])
            assert mp.off <= pg_off + KC * TL * 2

            sVg = [None] * 4
            tk.dma('sp', wsT_f.rearrange("p g q -> p (g q)"), w_sp[:, :], 'c2', w=['wsT_f'])
            tk.dma('sp', bs_all, b_sp[:, :], 'c2', w=['bs_all'])
            tk.dma('pool', wsT_b.rearrange("p g q -> p (g q)"), w_sp[:, :], 'c3', w=['wsT_b'])
            for b4 in range(4):
                if stage < 3.08:
                    continue
                tk.op('pe', lambda e, b4=b4: e.matmul(PS[1][:, :], lhsT=ones_f, rhs=wsT_f[:, 4 * b4:4 * b4 + 4, :], start=True, stop=True),
                      r=['wsT_f', 'ones_f'], w=['ps1'])
                for gi in range(4):
                    g = 4 * b4 + gi
                    tk.op('dve', lambda e, g=g, gi=gi: e.scalar_tensor_tensor(out=BB[:, g, :], in0=PS[1][:, gi * 128:(gi + 1) * 128], scalar=vcol(C_LNB + g),
                                                                              in1=bs_all[:, g * 128:(g + 1) * 128], op0=ALU.mult, op1=ALU.add),
                          r=['ps1', 'bs_all', 'vecsT'], w=['BB'])
            tk.barrier()
            sVg[0] = next_slot(colslot_src(w_in, 5120))
            sVg[1] = next_slot(colslot_src(w_in, 5120 + 512))
            if stage < 3.1:
                tk.barrier()
                break

            for cb in range(4):
                if cb + 2 < 4:
                    pass
                s = sVg[cb]
                for c in range(8):
                    bk = (cb * 8 + c) % 4

                    def mmg(e, bk=bk, c=c, s=s):
                        for k in range(KC):
                            ins = e.matmul(PS[bk][:, :], lhsT=hT[:, k, c * 128:(c + 1) * 128], rhs=slots[s][:, k, :], start=(k == 0), stop=(k == KC - 1))
                        return ins
                    tk.op('pe', mmg, r=['sl%d' % s, ('hT', (c // 4) * 512)], w=['ps%d' % bk])
                    tk.op('act', lambda e, bk=bk, c=c, cb=cb: e.activation(out=gvb[:, c, cb * 512:(cb + 1) * 512], in_=PS[bk][:, :], func=AF.Gelu),
                          r=['ps%d' % bk], w=[('gvb', c)])
                    tk.op('dve', lambda e, c=c, cb=cb: e.bn_stats(out=stats[:, c, cb, :], in_=gvb[:, c, cb * 512:(cb + 1) * 512]),
                          r=[('gvb', c)], w=[('stats', c)])
                if cb + 2 < 4:
                    sVg[cb + 2] = next_slot(colslot_src(w_in, 5120 + (cb + 2) * 512))
            sU = [next_slot(colslot_src(w_in, V_END)), None, None, None]
            for c in range(8):
                tk.op('dve', lambda e, c=c: e.bn_aggr(out=mv[:, c, :], in_=stats[:, c, :, :].rearrange("p a b -> p (a b)")),
                      r=[('stats', c)], w=['mv'])
            tk.op('act', lambda e: e.activation(out=lnr, in_=mv[:, :, 1], func=AF.Ln, bias=EPS, scale=1.0), r=['mv'], w=['lnr'])
            tk.op('act', lambda e: e.activation(out=rsd, in_=lnr, func=AF.Exp, scale=-0.5), r=['lnr'], w=['rsd'])
            tk.op('dve', lambda e: e.scalar_tensor_tensor(out=nbb, in0=mv[:, :, 0], scalar=-1.0, in1=rsd, op0=ALU.mult, op1=ALU.mult),
                  r=['mv', 'rsd'], w=['nbb'])
            for c in range(8):
                tk.op('act', lambda e, c=c: e.activation(out=gvb[:, c, :], in_=gvb[:, c, :], func=AF.Identity, bias=nbb[:, c:c + 1], scale=rsd[:, c:c + 1]),
                      r=[('gvb', c), 'rsd', 'nbb'], w=[('gvb', c)])
            if stage < 3.2:
                break
            for sb in range(4):
                if sb + 1 < 4:
                    sU[sb + 1] = next_slot(colslot_src(w_in, V_END + (sb + 1) * 512))
                s = sU[sb]
                for jj in range(4):
                    j = sb * 4 + jj
                    for bi, (t0, n, v) in enumerate(BLK_LAT):
                        bk = (jj * 2 + bi) % 4

                        def mmu(e, bk=bk, s=s, jj=jj, t0=t0, n=n):
                            for k in range(KC):
                                ins = e.matmul(PS[bk][:, 0:n], lhsT=slots[s][:, k, jj * 128:(jj + 1) * 128], rhs=hT[:, k, t0:t0 + n], start=(k == 0), stop=(k == KC - 1))
                            return ins
                        tk.op('pe', mmu, r=['sl%d' % s, ('hT', t0)], w=['ps%d' % bk])
                        tk.op('act', lambda e, bk=bk, j=j, t0=t0, n=n: e.activation(out=uT[:, j, t0:t0 + n], in_=PS[bk][:, 0:n], func=AF.Gelu),
                              r=['ps%d' % bk], w=[('uT', j, t0)])
            if stage < 3.3:
                break
            sW = next_slot(colslot_src(w_bg, 0))
            sG = next_slot(colslot_src(w_in, GV_END + D))
            for c in range(8):
                for gb in range(4):
                    bk = 4 + (c * 4 + gb) % 4

                    def mms(e, bk=bk, c=c, gb=gb):
                        for gi in range(4):
                            g = gb * 4 + gi
                            ins = e.matmul(PS[bk][:, gi * 128:(gi + 1) * 128], lhsT=gvb[:, c, g * 128:(g + 1) * 128], rhs=wsT_b[:, g, :], start=True, stop=True)
                        return ins
                    tk.op('pe', mms, r=[('gvb', c), 'wsT_b'], w=['ps%d' % bk])
                    for gi in range(4):
                        g = gb * 4 + gi
                        ti = (gi) % 2
                        tk.op('dve', lambda e, bk=bk, g=g, gi=gi, ti=ti: e.scalar_tensor_tensor(out=tg[ti], in0=PS[bk][:, gi * 128:(gi + 1) * 128], scalar=vcol(C_LNW + g),
                                                                                                   in1=BB[:, g, :], op0=ALU.mult, op1=ALU.add),
                              r=['ps%d' % bk, 'BB', 'vecsT'], w=['tg%d' % ti])
                        tk.op('dve', lambda e, g=g, c=c, ti=ti: e.tensor_tensor(out=uT[:, g, c * 128:(c + 1) * 128], in0=tg[ti], in1=uT[:, g, c * 128:(c + 1) * 128], op=ALU.mult),
                              r=['tg%d' % ti, ('uT', g, (c // 4) * 512)], w=[('uT', g, (c // 4) * 512)])
            if stage < 3.4:
                break
            ukeys = {t0: [('uT', j, t0) for j in range(KC)] for (t0, n, v) in BLK_LAT}
            for sb in range(4):
                for jj in range(4):
                    j = sb * 4 + jj
                    for bi, (t0, n, v) in enumerate(BLK_LAT):
                        st = (jj * 2 + bi) % 2
                        bp, bl = 2 * st, 2 * st + 1

                        def mmp(e, bp=bp, jj=jj, t0=t0, n=n, sW=sW):
                            for k in range(KC):
                                ins = e.matmul(PS[bp][:, 0:n], lhsT=slots[sW][:, k, jj * 128:(jj + 1) * 128], rhs=uT[:, k, t0:t0 + n], start=(k == 0), stop=(k == KC - 1))
                            return ins

                        def mml(e, bl=bl, jj=jj, t0=t0, n=n, sG=sG):
                            for k in range(KC):
                                ins = e.matmul(PS[bl][:, 0:n], lhsT=slots[sG][:, k, jj * 128:(jj + 1) * 128], rhs=hT[:, k, t0:t0 + n], start=(k == 0), stop=(k == KC - 1))
                            return ins
                        tk.op('pe', mml, r=['sl%d' % sG, ('hT', t0)], w=['ps%d' % bl])
                        tk.op('pe', mmp, r=['sl%d' % sW] + ukeys[t0], w=['ps%d' % bp])
                        tk.op('act', lambda e, bl=bl, st=st, j=j, n=n: e.activation(out=sg[st][:, 0:n], in_=PS[bl][:, 0:n], func=AF.Sigmoid, bias=vcol(C_BG + 16 + j), scale=1.0),
                              r=['ps%d' % bl, 'vecsT'], w=['sg%d' % st])
                        tk.op('dve', lambda e, bp=bp, st=st, j=j, t0=t0, n=n: e.tensor_tensor(out=pgT[:, j, t0:t0 + n], in0=PS[bp][:, 0:n], in1=sg[st][:, 0:n], op=ALU.mult),
                              r=['ps%d' % bp, 'sg%d' % st], w=['pgT'])
                if sb + 1 < 4:
                    sW = next_slot(colslot_src(w_bg, (sb + 1) * 512))
                    sG = next_slot(colslot_src(w_in, GV_END + D + (sb + 1) * 512))
            tk.dma('sp', pg_spill, pgT.rearrange("p h t -> p (h t)"), 'spill', r=['pgT'], w=['pg_spill'])
            if debug:
                tk.dma('sp', dbg['pgT'], pgT.rearrange("p h t -> p (h t)"), 'dbg', r=['pgT'])
            tk.barrier()

            if stage < 4:
                tk.barrier()
                break
            m3 = Map(PB)
            attnT = m3.get([128, KC, TL], BF16)
            qT = m3.get([128, KC, TL], BF16)
            kTs = [m3.get([128, NCORES, T], BF16) for _ in range(2)]
            kTc = [m3.get([128, 2 * 128], BF16) for _ in range(2)]
            vSs = [m3.get([128, NCORES, 8, 128], BF16) for _ in range(2)]
            vSc = [m3.get([128, 2, 128], BF16) for _ in range(2)]
            pT = [m3.get([128, 512], BF16) for _ in range(4)]
            rden = [m3.get([128, 512]) for _ in range(2)]
            tk.dma('sp', qT.rearrange("p h t -> p (h t)"), q_spill, 'ld', r=['q_spill'], w=['qT'])
            kall_v = kall.rearrange("(r h d) t -> d r h t", r=NCORES, h=4)
            vall_v = vall.rearrange("(r t) c -> t r c", r=NCORES)

            def load_kv(g):
                s = g % 2
                tk.dma('sp', kTs[s], kall_v[:, :, g, :], 'kt%d' % s, r=['kall'], w=['kT%d' % s])
                tk.dma('sp', kTc[s].rearrange("p (r t) -> p r t", r=NCORES), kall_v[:, :, g, TL:T], 'kt%d' % s, r=['kall'], w=['kT%d' % s])
                for r_ in range(NCORES):
                    tk.dma('sp', vSs[s][:, r_, :, :], vall_v[0:TL, r_, g * 128:(g + 1) * 128].rearrange("(c p) d -> p c d", p=128),
                           'vs%d' % s, r=['vall'], w=['vS%d' % s])
                for r_ in range(NCORES):
                    p0 = (r_ % 4) * TCX
                    tk.dma('sp', vSc[s][p0:p0 + TCX, r_ // 4, :], vall_v[TL:T, r_, g * 128:(g + 1) * 128], 'vs%d' % s, r=['vall'], w=['vS%d' % s])

            chunks = [(r_, c) for r_ in range(NCORES) for c in range(8)] + [(-1, 0), (-1, 1)]
            load_kv(0)
            it = 0
            scnt = 0
            for g in range(4):
                if g + 1 < 4:
                    load_kv(g + 1)
                s = g % 2
                for qt in range(8):
                    bO, bD = (3, 5) if it % 2 == 0 else (4, 6)
                    q4 = qT[:, 4 * g:4 * g + 4, qt * 128:(qt + 1) * 128]
                    nch = len(chunks)
                    sb_of = {}

                    def emit_S(ci):
                        nonlocal scnt
                        r_, c = chunks[ci]
                        bS = scnt % 3
                        pi = scnt % 4
                        scnt += 1
                        sb_of[ci] = (bS, pi)
                        kl = kTs[s][:, r_, c * 128:(c + 1) * 128] if r_ >= 0 else kTc[s][:, c * 128:(c + 1) * 128]
                        tk.op('pe', lambda e: e.matmul(PS[bS][:, :], lhsT=kl, rhs=q4, start=True, stop=True),
                              r=['kT%d' % s, 'qT'], w=['ps%d' % bS])
                        tk.op('act', lambda e: e.activation(out=pT[pi][:, :], in_=PS[bS][:, :], func=AF.Exp, scale=ATTN_SCALE),
                              r=['ps%d' % bS], w=['pT%d' % pi])

                    def emit_PV(ci):
                        r_, c = chunks[ci]
                        bS, pi = sb_of[ci]
                        vl = vSs[s][:, r_, c, :] if r_ >= 0 else vSc[s][:, c, :]
                        tk.op('pe', lambda e: e.matmul(PS[bO][:, :], lhsT=vl, rhs=pT[pi][:, :], start=(ci == 0), stop=(ci == nch - 1)),
                              r=['vS%d' % s, 'pT%d' % pi], w=['ps%d' % bO])
                        tk.op('pe', lambda e: e.matmul(PS[bD][:, :], lhsT=ones_b, rhs=pT[pi][:, :], start=(ci == 0), stop=(ci == nch - 1)),
                              r=['pT%d' % pi, 'ones_b'], w=['ps%d' % bD])

                    emit_S(0)
                    for ci in range(nch):
                        if ci + 1 < nch:
                            emit_S(ci + 1)
                        emit_PV(ci)
                    ri = it % 2
                    tk.op('dve', lambda e, ri=ri, bD=bD: e.reciprocal(out=rden[ri], in_=PS[bD][:, :]), r=['ps%d' % bD], w=['rden%d' % ri])
                    tk.op('dve', lambda e, ri=ri, bO=bO, g=g, qt=qt: e.tensor_tensor(out=attnT[:, 4 * g:4 * g + 4, qt * 128:(qt + 1) * 128],
                                                                                    in0=PS[bO][:, :].rearrange("p (a t) -> p a t", a=4),
                                                                                    in1=rden[ri].rearrange("p (a t) -> p a t", a=4), op=ALU.mult),
                          r=['ps%d' % bO, 'rden%d' % ri], w=['attnT'])
                    it += 1
            if debug:
                tk.dma('sp', dbg['attnT'], attnT.rearrange("p h t -> p (h t)"), 'dbg', r=['attnT'])
            tk.barrier()

            if stage < 5:
                break
            m4 = Map(PB + KC * T * 4)
            mergedT = m4.get([128, KC, TL], BF16)
            slots = [m4.get([128, KC, 512], BF16) for _ in range(3)]
            slot_n[0] = 0
            sg = [m4.get([128, 512]) for _ in range(2)]
            tb = [m4.get([128, 512]) for _ in range(2)]
            tk.dma('sp', mergedT.rearrange("p h t -> p (h t)"), pg_spill, 'ld', r=['pg_spill'], w=['mergedT'])
            sW = next_slot(colslot_src(w_ba, 0))
            sG = next_slot(colslot_src(w_in, GV_END))
            for sb in range(4):
                for jj in range(4):
                    j = sb * 4 + jj
                    for bi, (t0, n, v) in enumerate(BLK_LAT):
                        st = (jj * 2 + bi) % 2
                        bp, bl = 2 * st, 2 * st + 1

                        def mmp(e, bp=bp, jj=jj, t0=t0, n=n, sW=sW):
                            for k in range(KC):
                                ins = e.matmul(PS[bp][:, 0:n], lhsT=slots[sW][:, k, jj * 128:(jj + 1) * 128], rhs=attnT[:, k, t0:t0 + n], start=(k == 0), stop=(k == KC - 1))
                            return ins

                        def mml(e, bl=bl, jj=jj, t0=t0, n=n, sG=sG):
                            for k in range(KC):
                                ins = e.matmul(PS[bl][:, 0:n], lhsT=slots[sG][:, k, jj * 128:(jj + 1) * 128], rhs=hT[:, k, t0:t0 + n], start=(k == 0), stop=(k == KC - 1))
                            return ins
                        tk.op('pe', mml, r=['sl%d' % sG], w=['ps%d' % bl])
                        tk.op('pe', mmp, r=['sl%d' % sW], w=['ps%d' % bp])
                        tk.op('act', lambda e, bl=bl, st=st, j=j, n=n: e.activation(out=sg[st][:, 0:n], in_=PS[bl][:, 0:n], func=AF.Sigmoid, bias=vcol(C_BG + j), scale=1.0),
                              r=['ps%d' % bl, 'vecsT'], w=['sg%d' % st])
                        tk.op('dve', lambda e, bp=bp, st=st, n=n: e.tensor_tensor(out=tb[st][:, 0:n], in0=PS[bp][:, 0:n], in1=sg[st][:, 0:n], op=ALU.mult),
                              r=['ps%d' % bp, 'sg%d' % st], w=['tb%d' % st])
                        tk.op('dve', lambda e, st=st, j=j, t0=t0, n=n: e.tensor_tensor(out=mergedT[:, j, t0:t0 + n], in0=tb[st][:, 0:n], in1=mergedT[:, j, t0:t0 + n], op=ALU.add),
                              r=['tb%d' % st, 'mergedT'], w=['mergedT'])
                if sb + 1 < 4:
                    sW = next_slot(colslot_src(w_ba, (sb + 1) * 512))
                    sG = next_slot(colslot_src(w_in, GV_END + (sb + 1) * 512))
            tk.barrier()
            tk.dma('sp', xT.rearrange("p k t -> p (k t)"), x_spill, 'ld', r=['x_spill'], w=[('xT', i) for i in range(9)])
            slot_n[0] = 0
            sO = next_slot(colslot_src(w_o, 0))
            yc = 0
            for sb in range(4):
                sOn = next_slot(colslot_src(w_o, (sb + 1) * 512)) if sb + 1 < 4 else None
                for jj in range(4):
                    j = sb * 4 + jj
                    for bi, (t0, n, v) in enumerate(BLK_LAT):
                        bk = yc % 4
                        yc += 1

                        def mmo(e, bk=bk, jj=jj, t0=t0, n=n, sO=sO):
                            for k in range(KC):
                                ins = e.matmul(PS[bk][:, 0:n], lhsT=slots[sO][:, k, jj * 128:(jj + 1) * 128], rhs=mergedT[:, k, t0:t0 + n], start=(k == 0), stop=(k == KC - 1))
                            return ins
                        tk.op('pe', mmo, r=['sl%d' % sO], w=['ps%d' % bk])
                        tk.op('dve', lambda e, bk=bk, j=j, t0=t0, n=n: e.scalar_tensor_tensor(out=xT[:, j, t0:t0 + n], in0=PS[bk][:, 0:n], scalar=Pcol(6, j),
                                                                                               in1=xT[:, j, t0:t0 + n], op0=ALU.mult, op1=ALU.add),
                              r=['ps%d' % bk, 'PP'] + xkeys(t0, n), w=xkeys(t0, n))
                sO = sOn
            tk.barrier()
            if stage >= 9:
                norm_mod(BLK_LAT, (7, 7), 6)
                ffn(w2i, w2o, (8, 8), BLK_LAT)
        tk.barrier()
        if mode != 'A':
            final_out()
            tk.barrier()
    return nc


def _rope_tables():
    GRID_W = 64
    inv_freq = (10000.0 ** (-np.arange(0, 64, 2, dtype=np.float32) / np.float32(64))).astype(np.float32)
    tabs = []
    for c in range(NCORES):
        t = np.arange(c * TL, (c + 1) * TL)
        row = (t // GRID_W).astype(np.float32)
        col = (t % GRID_W).astype(np.float32)
        ang = np.concatenate([row[:, None] * inv_freq, col[:, None] * inv_freq], axis=-1).astype(np.float32)
        cos = np.cos(ang).astype(np.float32)
        sin = np.sin(ang).astype(np.float32)
        cosT = np.ones((128, T), np.float32)
        sinT = np.zeros((128, T), np.float32)
        cosT[:, :TL] = np.repeat(cos.T, 2, axis=0)
        sinT[:, :TL] = np.repeat(sin.T, 2, axis=0)
        tabs.append((cosT, sinT))
    return tabs


def make_in_maps(x, c, ctx, c_ctx, w_mod, b_mod, norm_w, w_ffn1_in, w_ffn1_out, w_ffn2_in, w_ffn2_out,
                 w_in, b_gate, q_norm_w, k_norm_w, gmlp_ln_w, gmlp_ln_b, w_spatial, b_spatial,
                 w_branch_attn, w_branch_gmlp, w_out, final_norm_w):
    f = lambda a: np.ascontiguousarray(np.asarray(a, dtype=np.float32))
    vecs = np.zeros((NVEC, 128), np.float32)
    vecs[C_BMOD:C_BMOD + 144] = f(b_mod).reshape(144, 128)
    vecs[C_NW:C_NW + 48] = f(norm_w).reshape(48, 128)
    vecs[C_FNW:C_FNW + 16] = f(final_norm_w).reshape(16, 128)
    vecs[C_BG:C_BG + 32] = f(b_gate).reshape(32, 128)
    vecs[C_LNW:C_LNW + 16] = f(gmlp_ln_w).reshape(16, 128)
    vecs[C_LNB:C_LNB + 16] = f(gmlp_ln_b).reshape(16, 128)
    vecs[C_C:C_C + 16] = f(c).reshape(16, 128)
    vecs[C_CC:C_CC + 16] = f(c_ctx).reshape(16, 128)
    gq = f(q_norm_w).reshape(128)
    gk = f(k_norm_w).reshape(128)
    swap = np.arange(128) ^ 1
    vecs[C_GQ] = gq; vecs[C_GQS] = gq[swap]; vecs[C_GK] = gk; vecs[C_GKS] = gk[swap]
    ident = np.eye(128, dtype=np.float32)
    perm = np.zeros((128, 128), np.float32)
    for i in range(64):
        perm[2 * i + 1, 2 * i] = -1.0
        perm[2 * i, 2 * i + 1] = 1.0
    tabs = _rope_tables()
    xf = f(x).reshape(NCORES * TL, D)
    cf = f(ctx).reshape(NCORES * TCX, D)
    shared = {
        "vecs": vecs, "w_mod": f(w_mod).reshape(D, 9 * D), "w_ffn1_in": f(w_ffn1_in).reshape(D, 2 * FF),
        "w_ffn1_out": f(w_ffn1_out).reshape(FF, D), "w_ffn2_in": f(w_ffn2_in).reshape(D, 2 * FF),
        "w_ffn2_out": f(w_ffn2_out).reshape(FF, D), "w_in": f(w_in).reshape(D, 11264),
        "w_branch_attn": f(w_branch_attn).reshape(D, D), "w_branch_gmlp": f(w_branch_gmlp).reshape(D, D),
        "w_out": f(w_out).reshape(D, D),
        "w_sp2": np.ascontiguousarray(f(w_spatial).reshape(16, 128, 128).transpose(2, 0, 1).reshape(128, 2048)),
        "b_sp_bc": np.ascontiguousarray(np.broadcast_to(f(b_spatial).reshape(1, 2048), (128, 2048))),
        "ident": ident, "perm": perm,
    }
    maps = []
    for i in range(NCORES):
        m = dict(shared)
        m["x_loc"] = xf[i * TL:(i + 1) * TL]
        m["ctx_loc"] = cf[i * TCX:(i + 1) * TCX]
        m["cosT"], m["sinT"] = tabs[i]
        maps.append(m)
    return maps


_NC_CACHE = {}
FUSED = False


STAGE_B = 9


def _get_nc(mode):
    if mode not in _NC_CACHE:
        _NC_CACHE[mode] = build_nc(stage=(STAGE_B if mode == 'B' else 9), debug=False, mode=mode)
    return _NC_CACHE[mode]


def kernel(**inputs):
    maps = make_in_maps(**inputs)
    cores = list(range(NCORES))
    if FUSED:
        res = run_bass_kernel_spmd(_get_nc('F'), maps, core_ids=cores)
    else:
        a_keys = ("x_loc", "ctx_loc", "vecs", "w_mod", "w_ffn1_in", "w_ffn1_out", "w_in", "cosT", "sinT", "ident", "perm")
        ra = run_bass_kernel_spmd(_get_nc('A'), [{k: m[k] for k in a_keys} for m in maps], core_ids=cores).results
        kall = np.concatenate([np.asarray(r["kst"]) for r in ra], axis=0)
        vall = np.concatenate([np.asarray(r["vst"]) for r in ra], axis=0)
        b_drop = ("x_loc", "ctx_loc", "w_mod", "w_ffn1_in", "w_ffn1_out")
        mb = []
        for i, m in enumerate(maps):
            d = {k: v for k, v in m.items() if k not in b_drop}
            d["x_spill"] = np.asarray(ra[i]["x_spill"])
            d["modT_i"] = np.asarray(ra[i]["modT_o"])
            d["kall"] = kall
            d["vall"] = vall
            mb.append(d)
        res = run_bass_kernel_spmd(_get_nc('B'), mb, core_ids=cores)
    out = np.concatenate([np.asarray(r["out_loc"], dtype=np.float32) for r in res.results], axis=0)
    return out.reshape(1, NCORES * TL, D)
```
